# Optimizing a Trainium2 kernel written in Bass

```python
import math
import jax, jax.numpy as jnp
from jax import lax
import numpy as np

D_MODEL = 1024
BATCH = 4
SEQ = 8192
DEPTH = 2

SSM_WIDTH = D_MODEL // 2
SSM_GROUP = 16
SSM_GROUPS = SSM_WIDTH // SSM_GROUP
SSM_STATE = 64
SSM_CHUNK_MAX = 512
DT_MIN = 1e-3
DT_MAX = 1e-1
ATTN_HEADS = 8
HEAD_DIM = 64
ATTN_WIDTH = ATTN_HEADS * HEAD_DIM
IDX_HEADS = 8
IDX_DIM = 64
IDX_SCALE = (IDX_HEADS * IDX_DIM) ** -0.5
TOPK_MAX = 256
Q_BLOCK = 128
RMS_EPS = 1e-6

IN_WIDTHS = (
    SSM_WIDTH,
    SSM_WIDTH,
    ATTN_WIDTH,
    HEAD_DIM,
    HEAD_DIM,
    ATTN_WIDTH,
    IDX_HEADS * IDX_DIM,
    IDX_DIM,
    IDX_HEADS,
    D_MODEL,
    D_MODEL,
)
IN_COLS = sum(IN_WIDTHS)
IN_SPLITS = tuple(int(s) for s in np.cumsum(IN_WIDTHS)[:-1])

kernel_name = "hybrid_s5_dsa_gated_parallel"


def rms_norm(x, g):
    x32 = x.astype(jnp.float32)
    y = x32 * lax.rsqrt(jnp.mean(x32 * x32, axis=-1, keepdims=True) + RMS_EPS)
    return (y * g.astype(jnp.float32)).astype(x.dtype)


def _complex_affine_combine(e1, e2):
    a1r, a1i, b1r, b1i = e1
    a2r, a2i, b2r, b2i = e2
    return (a1r * a2r - a1i * a2i,
            a1r * a2i + a1i * a2r,
            a2r * b1r - a2i * b1i + b2r,
            a2r * b1i + a2i * b1r + b2i)


def s5_mixer(u, a_re, a_im, b_re, b_im, c_re, c_im, d_skip, log_dt):
    bsz, seq_len, _ = u.shape
    f32 = jnp.float32
    ar, ai = a_re.astype(f32), a_im.astype(f32)
    dt = jnp.exp(log_dt.astype(f32))[:, None]
    mag = jnp.exp(ar * dt)
    abar_r, abar_i = mag * jnp.cos(ai * dt), mag * jnp.sin(ai * dt)
    den = ar * ar + ai * ai
    nr = abar_r - 1.0
    cr = (nr * ar + abar_i * ai) / den
    ci = (abar_i * ar - nr * ai) / den
    br, bi = b_re.astype(f32), b_im.astype(f32)
    bb_r = cr[..., None] * br - ci[..., None] * bi
    bb_i = cr[..., None] * bi + ci[..., None] * br
    cre, cim = c_re.astype(f32), c_im.astype(f32)

    chunk = math.gcd(seq_len, SSM_CHUNK_MAX)
    n_chunks = seq_len // chunk
    ug = u.astype(f32).reshape(bsz, n_chunks, chunk, SSM_GROUPS, SSM_GROUP)
    ug = ug.transpose(1, 0, 2, 3, 4)

    def chunk_step(h0, u_c):
        h0r, h0i = h0
        bu_r = jnp.einsum('btgp,gnp->btgn', u_c, bb_r)
        bu_i = jnp.einsum('btgp,gnp->btgn', u_c, bb_i)
        a_r = jnp.broadcast_to(abar_r, bu_r.shape)
        a_i = jnp.broadcast_to(abar_i, bu_i.shape)
        cum_r, cum_i, hl_r, hl_i = lax.associative_scan(
            _complex_affine_combine, (a_r, a_i, bu_r, bu_i), axis=1)
        h_r = hl_r + cum_r * h0r[:, None] - cum_i * h0i[:, None]
        h_i = hl_i + cum_r * h0i[:, None] + cum_i * h0r[:, None]
        y = (jnp.einsum('btgn,gpn->btgp', h_r, cre)
             - jnp.einsum('btgn,gpn->btgp', h_i, cim))
        return (h_r[:, -1], h_i[:, -1]), y

    h_init = (jnp.zeros((bsz, SSM_GROUPS, SSM_STATE), f32),
              jnp.zeros((bsz, SSM_GROUPS, SSM_STATE), f32))
    _, y = lax.scan(chunk_step, h_init, ug)
    y = y.transpose(1, 0, 2, 3, 4).reshape(bsz, seq_len, SSM_WIDTH)
    y = y + d_skip.astype(f32) * u.astype(f32)
    return y.astype(u.dtype)


def dsa_mixer(q, k, v, q_idx, k_idx, w_idx):
    bsz, seq_len = q.shape[0], q.shape[1]
    f32 = jnp.float32
    top_k = min(TOPK_MAX, seq_len // 4)
    n_blocks = seq_len // Q_BLOCK
    key_pos = jnp.arange(seq_len)
    k_idx32 = k_idx.astype(f32)
    attn_scale = HEAD_DIM ** -0.5

    def to_blocks(t):
        return t.reshape((bsz, n_blocks, Q_BLOCK) + t.shape[2:]).swapaxes(0, 1)

    def block(args):
        blk, qb, qib, wb = args
        q_pos = blk * Q_BLOCK + jnp.arange(Q_BLOCK)
        causal = key_pos[None, :] <= q_pos[:, None]
        s = jax.nn.relu(jnp.einsum('bqhd,bkd->bqhk', qib.astype(f32), k_idx32)) * IDX_SCALE
        score = jnp.einsum('bqh,bqhk->bqk', wb.astype(f32), s)
        score = jnp.where(causal[None], score, -jnp.inf)
        _, sel = lax.top_k(score, top_k)
        valid = sel <= q_pos[None, :, None]
        k_sel = jax.vmap(lambda kk, ii: kk[ii])(k, sel)
        v_sel = jax.vmap(lambda vv, ii: vv[ii])(v, sel)
        logits = jnp.einsum('bqhd,bqkd->bqhk', qb.astype(f32), k_sel.astype(f32)) * attn_scale
        logits = jnp.where(valid[:, :, None, :], logits, -jnp.inf)
        p = jax.nn.softmax(logits, axis=-1)
        o = jnp.einsum('bqhk,bqkd->bqhd', p, v_sel.astype(f32))
        return o.astype(qb.dtype)

    out = lax.map(block, (jnp.arange(n_blocks), to_blocks(q), to_blocks(q_idx), to_blocks(w_idx)))
    return out.swapaxes(0, 1).reshape(bsz, seq_len, ATTN_WIDTH)


def hybrid_layer(x, c, norm_g, w_mod, b_mod, w_in, a_re, a_im, b_re, b_im, c_re, c_im,
                 d_skip, log_dt, w_glu, b_glu, w_a_o, w_b_o, w_out):
    bsz, seq_len, _ = x.shape
    mod = jax.nn.silu(c) @ w_mod + b_mod
    shift, scale, gate = jnp.split(mod, 3, axis=-1)
    h = rms_norm(x, norm_g) * (1 + scale[:, None]) + shift[:, None]
    proj = h @ w_in
    xa, za, q, k, v, zb, qi, ki, wi, ga, gb = jnp.split(proj, IN_SPLITS, axis=-1)

    ya = jax.nn.gelu(s5_mixer(xa, a_re, a_im, b_re, b_im, c_re, c_im, d_skip, log_dt))
    g = ya @ w_glu + b_glu
    ya = g[..., :SSM_WIDTH] * jax.nn.sigmoid(g[..., SSM_WIDTH:])
    ya = (ya * jax.nn.silu(za)) @ w_a_o

    yb = dsa_mixer(q.reshape(bsz, seq_len, ATTN_HEADS, HEAD_DIM), k, v,
                   qi.reshape(bsz, seq_len, IDX_HEADS, IDX_DIM), ki, wi)
    yb = (yb * jax.nn.silu(zb)) @ w_b_o

    merged = jax.nn.sigmoid(ga) * ya + jax.nn.sigmoid(gb) * yb
    return x + gate[:, None] * (merged @ w_out)


def setup_inputs(seed: int = 0) -> dict:
    key = jax.random.key(seed)
    ks = jax.random.split(key, 20)
    f32 = jnp.float32
    G, N, P = SSM_GROUPS, SSM_STATE, SSM_GROUP

    def nrm(k, shape, std):
        return jax.random.normal(k, shape, f32) * std

    n_idx = jnp.arange(N, dtype=f32)
    return {
        "x": nrm(ks[0], (BATCH, SEQ, D_MODEL), 1.0),
        "c": nrm(ks[1], (BATCH, D_MODEL), 1.0),
        "norm_g": 1.0 + nrm(ks[2], (DEPTH, D_MODEL), 0.02),
        "w_mod": nrm(ks[3], (DEPTH, D_MODEL, 3 * D_MODEL), 0.5 * D_MODEL ** -0.5),
        "b_mod": nrm(ks[4], (DEPTH, 3 * D_MODEL), 0.01),
        "w_in": nrm(ks[5], (DEPTH, D_MODEL, IN_COLS), D_MODEL ** -0.5),
        "a_re": -0.5 + nrm(ks[6], (DEPTH, G, N), 0.01),
        "a_im": math.pi * n_idx + nrm(ks[7], (DEPTH, G, N), 0.01),
        "b_re": nrm(ks[8], (DEPTH, G, N, P), (2 * P) ** -0.5),
        "b_im": nrm(ks[9], (DEPTH, G, N, P), (2 * P) ** -0.5),
        "c_re": nrm(ks[10], (DEPTH, G, P, N), N ** -0.5),
        "c_im": nrm(ks[11], (DEPTH, G, P, N), N ** -0.5),
        "d_skip": nrm(ks[12], (DEPTH, SSM_WIDTH), 1.0),
        "log_dt": jax.random.uniform(ks[13], (DEPTH, G), f32,
                                     minval=math.log(DT_MIN), maxval=math.log(DT_MAX)),
        "w_glu": nrm(ks[14], (DEPTH, SSM_WIDTH, 2 * SSM_WIDTH), SSM_WIDTH ** -0.5),
        "b_glu": nrm(ks[15], (DEPTH, 2 * SSM_WIDTH), 0.01),
        "w_a_o": nrm(ks[16], (DEPTH, SSM_WIDTH, D_MODEL), SSM_WIDTH ** -0.5),
        "w_b_o": nrm(ks[17], (DEPTH, ATTN_WIDTH, D_MODEL), ATTN_WIDTH ** -0.5),
        "w_out": nrm(ks[18], (DEPTH, D_MODEL, D_MODEL), D_MODEL ** -0.5),
        "final_g": 1.0 + nrm(ks[19], (D_MODEL,), 0.02),
    }


def reference(x, c, norm_g, w_mod, b_mod, w_in, a_re, a_im, b_re, b_im, c_re, c_im,
              d_skip, log_dt, w_glu, b_glu, w_a_o, w_b_o, w_out, final_g):
    for l in range(DEPTH):
        x = hybrid_layer(x, c, norm_g[l], w_mod[l], b_mod[l], w_in[l], a_re[l], a_im[l],
                         b_re[l], b_im[l], c_re[l], c_im[l], d_skip[l], log_dt[l],
                         w_glu[l], b_glu[l], w_a_o[l], w_b_o[l], w_out[l])
    return rms_norm(x, final_g)
```

```python
import math
import os
from contextlib import ExitStack
import numpy as np
import concourse.bass as bass
import concourse.mybir as mybir
from concourse.bass_utils import run_bass_kernel_spmd

F32 = mybir.dt.float32
BF16 = mybir.dt.bfloat16
ALU = mybir.AluOpType
AF = mybir.ActivationFunctionType
AX = mybir.AxisListType

D = 1024
NB = 4
DEPTH = 2
SW = 512
NG = 32
GP = 16
NS = 64
INC = 4808
TOPK = 256
IDX_SCALE = 512.0 ** -0.5
EPS = 1e-6
C_XA, C_ZA, C_Q, C_K, C_V, C_ZB, C_QI, C_KI, C_WI, C_GA, C_GB = 0, 512, 1024, 1536, 1600, 1664, 2176, 2688, 2752, 2760, 3784
NEG = -1.0e30
ENGS = ("pe", "act", "dve", "pool", "sp")


class Buf:
    __slots__ = ("w", "r", "re")

    def __init__(self):
        self.w = None
        self.r = {}
        self.re = -1


class Sched:
    def __init__(self, nc, es, ndma=4):
        self.nc = nc
        self.es = es
        self.ndma = ndma
        self.th = {e: [] for e in ENGS}
        self.epoch = -1
        self.new_epoch()

    def new_epoch(self):
        nc, es = self.nc, self.es
        self.epoch += 1
        ep = self.epoch
        self.cnt = {e: 0 for e in ENGS}
        self.sem = {e: es.enter_context(nc.semaphore("s%d_%s" % (ep, e))) for e in ENGS}
        self.waited = {}
        self.dq = {}
        for q in ("sp", "pool"):
            self.dq[q] = {"sems": [es.enter_context(nc.semaphore("d%d_%s%d" % (ep, q, i))) for i in range(self.ndma)],
                          "val": [0] * self.ndma, "nxt": 0}
        for q in self.dq:
            for i, sm_ in enumerate(self.dq[q]["sems"]):
                self.sem[("d", q, i)] = sm_

    def _deps(self, eng, R, W):
        need = {}
        ep = self.epoch
        for b in R:
            if b.w is not None and b.w[2] == ep:
                need[b.w[0]] = max(need.get(b.w[0], 0), b.w[1])
        for b in W:
            if b.w is not None and b.w[2] == ep:
                need[b.w[0]] = max(need.get(b.w[0], 0), b.w[1])
            if b.re == ep:
                for k, v in b.r.items():
                    need[k] = max(need.get(k, 0), v)
        out = []
        for k, v in need.items():
            if k == eng and eng == "pe":
                continue
            if self.waited.get((eng, k), 0) >= v:
                continue
            self.waited[(eng, k)] = v
            out.append((self.sem[k], v))
        return out

    def _mark(self, tok, R, W):
        ep = self.epoch
        for b in W:
            b.w = (tok[0], tok[1], ep)
            b.r = {}
            b.re = ep
        for b in R:
            if b.re != ep:
                b.r = {}
                b.re = ep
            if b.r.get(tok[0], 0) < tok[1]:
                b.r[tok[0]] = tok[1]

    def op(self, eng, fn, R=(), W=()):
        waits = self._deps(eng, R, W)
        self.cnt[eng] += 1
        sem = self.sem[eng]

        def th(e):
            for s, v in waits:
                e.wait_ge(s, v)
            fn(e).then_inc(sem, 1)
        self.th[eng].append(th)
        self._mark((eng, self.cnt[eng]), R, W)

    def dma(self, q, out, in_, R=(), W=(), **kw):
        d = self.dq[q]
        i = d["nxt"]
        d["nxt"] = (i + 1) % len(d["sems"])
        key = ("d", q, i)
        waits = self._deps(q, R, W)
        prev = d["val"][i]
        if prev > 0 and self.waited.get((q, key), 0) < prev:
            self.waited[(q, key)] = prev
            waits.append((d["sems"][i], prev))
        d["val"][i] = prev + 16
        sem = d["sems"][i]

        def th(e):
            for s, v in waits:
                e.wait_ge(s, v)
            e.dma_start(out=out, in_=in_, **kw).then_inc(sem, 16)
        self.th[q].append(th)
        self._mark((key, prev + 16), R, W)

    def barrier(self):
        targets = [(e, self.cnt[e]) for e in ENGS if self.cnt[e] > 0]
        for q, d in self.dq.items():
            for i, v in enumerate(d["val"]):
                if v > 0:
                    targets.append((("d", q, i), v))
        for e in ENGS:
            ws = []
            for k, v in targets:
                if k == e:
                    continue
                if self.waited.get((e, k), 0) >= v:
                    continue
                self.waited[(e, k)] = v
                ws.append((self.sem[k], v))

            def th(eng, ws=ws):
                for s, v in ws:
                    eng.wait_ge(s, v)
            self.th[e].append(th)

    def run(self):
        nc = self.nc
        self.barrier()
        th = self.th
        with nc.Block() as block:
            @block.tensor
            def _(e):
                for t in th["pe"]:
                    t(e)

            @block.scalar
            def _(e):
                for t in th["act"]:
                    t(e)

            @block.vector
            def _(e):
                for t in th["dve"]:
                    t(e)

            @block.gpsimd
            def _(e):
                for t in th["pool"]:
                    t(e)

            @block.sync
            def _(e):
                for t in th["sp"]:
                    t(e)
        self.th = {e: [] for e in ENGS}


class Ctx:
    pass


_UID = [0]


def _alloc(nc, es, bufs, name, shape, dt):
    _UID[0] += 1
    t = es.enter_context(nc.sbuf_tensor("%s_u%d" % (name, _UID[0]), list(shape), dt))
    b = Buf()
    bufs[name] = b
    return t, b


def build(L=8192, depth=DEPTH, J=32, nit=16, debug=False, phases="ASDO"):
    NSC = L // (128 * J)
    assert NSC * 128 * J == L
    JP = J * GP
    KC = JP // 128
    NQB = L // 128
    NST = L // 512
    nc = bass.Bass("TRN2", target_bir_lowering=False)
    okind = "ExternalOutput" if debug else "Internal"

    def din(name, shape, dt=F32):
        return nc.dram_tensor(name, list(shape), dt, kind="ExternalInput").ap()

    x_in = din("x", [L, D])
    c_lay = din("c_lay", [128, 8])
    normg_lay = din("normg_lay", [depth, 128, 8])
    w_mod = din("w_mod", [depth, D, 3 * D])
    b_mod = din("b_mod", [depth, 3 * D])
    w_in = din("w_in", [depth, D, INC])
    a_re = din("a_re", [depth, NG, NS])
    a_im = din("a_im", [depth, NG, NS])
    b_re = din("b_re", [depth, NG, NS, GP])
    b_im = din("b_im", [depth, NG, NS, GP])
    c_re = din("c_re", [depth, NG * GP, NS])
    c_im = din("c_im", [depth, NG * GP, NS])
    d_skip = din("d_skip", [depth, SW])
    log_dt = din("log_dt", [depth, NG, 1])
    w_glu = din("w_glu", [depth, SW, 2 * SW])
    b_glu = din("b_glu", [depth, 2 * SW])
    w_a_o = din("w_a_o", [depth, SW, D])
    w_b_o = din("w_b_o", [depth, SW, D])
    w_out = din("w_out", [depth, D, D])
    final_g = din("final_g", [D])
    ident_in = din("ident", [128, 128])
    causal_in = din("causal", [128, 128])
    kmask_in = din("kmask", [128, KC, JP])
    mtab_in = din("mtab", [128, 3 * J + 1])

    out = nc.dram_tensor("out", [L, D], F32, kind="ExternalOutput").ap()
    x1 = nc.dram_tensor("x1", [L, D], F32, kind=okind).ap()
    xa_s = nc.dram_tensor("xa_s", [L, SW], F32, kind=okind).ap()
    tm_s = nc.dram_tensor("tm_s", [L, 3136], BF16, kind=okind).ap()
    wi_s = nc.dram_tensor("wi_s", [L, 8], F32, kind=okind).ap()
    fm_s = nc.dram_tensor("fm_s", [128, 10, L], BF16, kind=okind).ap()
    ys_s = nc.dram_tensor("ys_s", [L, SW], F32, kind=okind).ap()
    o_s = nc.dram_tensor("o_s", [L, SW], BF16, kind=okind).ap()
    dbufs = {}

    def db(*key):
        if key not in dbufs:
            dbufs[key] = Buf()
        return dbufs[key]

    with ExitStack() as ges:
        S = Sched(nc, ges)
        gb = {}
        ident_f, b_idf = _alloc(nc, ges, gb, "ident_f", [128, 128], F32)
        ident_b, b_idb = _alloc(nc, ges, gb, "ident_b", [128, 128], BF16)
        gateB, b_gate = _alloc(nc, ges, gb, "gateB", [128, D], F32)
        ones_r, b_ones = _alloc(nc, ges, gb, "ones_r", [1, 128], F32)
        pbt = [ges.enter_context(nc.psum_tensor("pb%d" % i, [128, 512], F32)) for i in range(8)]
        pbb = [Buf() for _ in range(8)]

        S.dma("sp", ident_f[:], ident_in[:, :], W=[b_idf])
        S.op("dve", lambda e: e.tensor_copy(out=ident_b[:], in_=ident_f[:]), R=[b_idf], W=[b_idb])
        S.op("dve", lambda e: e.memset(ones_r[:], 1.0), W=[b_ones])

        def phase_A(l):
            xsrc = x_in if l == 0 else x1
            with ExitStack() as es:
                B = {}
                w_bf, b_w = _alloc(nc, es, B, "w_bf", [128, 8, INC], BF16)
                wst = [_alloc(nc, es, B, "wst%d" % i, [128, 1202], F32) for i in range(2)]
                wmt = [_alloc(nc, es, B, "wmt%d" % i, [128, 1536], F32) for i in range(2)]
                wdup, b_wdup = _alloc(nc, es, B, "wdup", [128, 8, 256], BF16)
                cT, b_cT = _alloc(nc, es, B, "cT", [128, 8], F32)
                sc, b_sc = _alloc(nc, es, B, "sc", [128, 8], F32)
                ngT, b_ngT = _alloc(nc, es, B, "ngT", [128, 8], F32)
                modrow, b_modrow = _alloc(nc, es, B, "modrow", [1, 3 * D], F32)
                bmrow, b_bmrow = _alloc(nc, es, B, "bmrow", [1, 3 * D], F32)
                modT, b_modT = _alloc(nc, es, B, "modT", [128, 24], F32)
                gmulT, b_gmulT = _alloc(nc, es, B, "gmulT", [128, 8], F32)
                xt = [_alloc(nc, es, B, "xt%d" % i, [128, D], F32) for i in range(2)]
                xs = [_alloc(nc, es, B, "xs%d" % i, [128, D], F32) for i in range(2)]
                junk, b_junk = _alloc(nc, es, B, "junkA", [128, D], BF16)
                st8 = [_alloc(nc, es, B, "st8_%d" % i, [128, 8], F32) for i in range(2)]
                hT = [_alloc(nc, es, B, "hT%d" % i, [128, 8, 512], BF16) for i in range(2)]
                fmst = [_alloc(nc, es, B, "fmst%d" % i, [128, 10, 512], BF16) for i in range(1)]
                tmst = [_alloc(nc, es, B, "tmst%d" % i, [128, 3136], BF16) for i in range(2)]
                xast = [_alloc(nc, es, B, "xast%d" % i, [128, 512], F32) for i in range(2)]
                wist = [_alloc(nc, es, B, "wist%d" % i, [128, 8], F32) for i in range(2)]

                for k in range(8):
                    for hf in range(4):
                        t, b = wst[hf % 2]
                        S.dma("sp", t[:], w_in[l, k * 128:(k + 1) * 128, hf * 1202:(hf + 1) * 1202], W=[b])
                        if hf % 2 == 0:
                            S.op("dve", lambda e, t=t, k=k, hf=hf: e.tensor_copy(out=w_bf[:, k, hf * 1202:(hf + 1) * 1202], in_=t[:]), R=[b], W=[b_w])
                        else:
                            S.op("act", lambda e, t=t, k=k, hf=hf: e.activation(out=w_bf[:, k, hf * 1202:(hf + 1) * 1202], in_=t[:], func=AF.Copy), R=[b], W=[b_w])
                for k in range(8):
                    S.op("dve", lambda e, k=k: e.tensor_copy(out=wdup[:, k, 0:64], in_=w_bf[:, k, C_K:C_K + 64]), R=[b_w], W=[b_wdup])
                    S.op("dve", lambda e, k=k: e.tensor_copy(out=wdup[:, k, 64:128], in_=w_bf[:, k, C_K:C_K + 64]), R=[b_w], W=[b_wdup])
                    S.op("dve", lambda e, k=k: e.tensor_copy(out=wdup[:, k, 128:192], in_=w_bf[:, k, C_KI:C_KI + 64]), R=[b_w], W=[b_wdup])
                    S.op("dve", lambda e, k=k: e.tensor_copy(out=wdup[:, k, 192:256], in_=w_bf[:, k, C_KI:C_KI + 64]), R=[b_w], W=[b_wdup])

                S.dma("sp", cT[:], c_lay[:, :], W=[b_cT])
                S.dma("sp", ngT[:], normg_lay[l, :, :], W=[b_ngT])
                S.dma("sp", bmrow[:], b_mod[l:l + 1, :], W=[b_bmrow])
                S.op("act", lambda e: e.activation(out=sc[:], in_=cT[:], func=AF.Silu), R=[b_cT], W=[b_sc])
                for k in range(8):
                    for hh in range(2):
                        t, b = wmt[hh]
                        S.dma("sp", t[:], w_mod[l, k * 128:(k + 1) * 128, hh * 1536:(hh + 1) * 1536], W=[b])
                        for jj in range(3):
                            j = hh * 3 + jj
                            S.op("pe", lambda e, t=t, k=k, j=j, jj=jj: e.matmul(pbt[j][0:1, :], lhsT=sc[:, k:k + 1], rhs=t[:, jj * 512:(jj + 1) * 512], start=(k == 0), stop=(k == 7)),
                                 R=[b, b_sc], W=[pbb[j]])
                for j in range(6):
                    S.op("dve", lambda e, j=j: e.tensor_tensor(out=modrow[0:1, j * 512:(j + 1) * 512], in0=pbt[j][0:1, :], in1=bmrow[0:1, j * 512:(j + 1) * 512], op=ALU.add),
                         R=[pbb[j], b_bmrow], W=[b_modrow])
                for j in range(24):
                    S.op("pe", lambda e, j=j: e.matmul(pbt[6][:, j:j + 1], lhsT=modrow[0:1, j * 128:(j + 1) * 128], rhs=ones_r[0:1, 0:1], start=True, stop=True),
                         R=[b_modrow, b_ones], W=[pbb[6]])
                S.op("dve", lambda e: e.tensor_copy(out=modT[:], in_=pbt[6][:, 0:24]), R=[pbb[6]], W=[b_modT])
                S.op("dve", lambda e: e.tensor_scalar(out=gmulT[:], in0=modT[:, 8:16], scalar1=1.0, scalar2=None, op0=ALU.add), R=[b_modT], W=[b_gmulT])
                S.op("dve", lambda e: e.tensor_tensor(out=gmulT[:], in0=gmulT[:], in1=ngT[:], op=ALU.mult), R=[b_gmulT, b_ngT], W=[b_gmulT])
                for hf in range(2):
                    S.op("pe", lambda e, hf=hf: e.matmul(pbt[7][:, :], lhsT=ones_r[0:1, :], rhs=modrow[0:1, 2048 + hf * 512:2048 + (hf + 1) * 512], start=True, stop=True),
                         R=[b_modrow, b_ones], W=[pbb[7]])
                    S.op("dve", lambda e, hf=hf: e.tensor_copy(out=gateB[:, hf * 512:(hf + 1) * 512], in_=pbt[7][:, :]), R=[pbb[7]], W=[b_gate])

                tm_groups = [(C_ZA, 512, 0, AF.Silu), (C_ZB, 512, 512, AF.Silu), (C_GA, 512, 1024, AF.Sigmoid), (C_GA + 512, 512, 1536, AF.Sigmoid),
                             (C_GB, 512, 2048, AF.Sigmoid), (C_GB + 512, 512, 2560, AF.Sigmoid)]
                rot = [0]

                def nbank():
                    rot[0] = (rot[0] + 1) % 4
                    return 4 + rot[0]

                for st in range(NST):
                    hTt, b_hT = hT[st % 2]
                    fmt, b_fm = fmst[0]
                    for j in range(4):
                        ti = st * 4 + j
                        tok0 = ti * 128
                        xtt, b_xt = xt[ti % 2]
                        xst, b_xs = xs[ti % 2]
                        s8, b_s8 = st8[ti % 2]
                        tmt, b_tm = tmst[ti % 2]
                        xat, b_xa = xast[ti % 2]
                        wit, b_wi = wist[ti % 2]
                        S.dma("sp", xtt[:], xsrc[tok0:tok0 + 128, :], R=[db("x", l, ti)], W=[b_xt])
                        S.op("dve", lambda e, s8=s8: e.memset(s8[:], 0.0), W=[b_s8])
                        S.op("act", lambda e, xtt=xtt, s8=s8: e.activation(out=junk[:], in_=xtt[:], func=AF.Square, accum_out=s8[:, 0:1]), R=[b_xt, b_s8], W=[b_junk, b_s8])
                        S.op("dve", lambda e, s8=s8: e.tensor_scalar(out=s8[:, 1:2], in0=s8[:, 0:1], scalar1=1.0 / D, scalar2=EPS, op0=ALU.mult, op1=ALU.add), R=[b_s8], W=[b_s8])
                        S.op("act", lambda e, s8=s8: e.activation(out=s8[:, 2:3], in_=s8[:, 1:2], func=AF.Sqrt), R=[b_s8], W=[b_s8])
                        S.op("dve", lambda e, s8=s8: e.reciprocal(out=s8[:, 3:4], in_=s8[:, 2:3]), R=[b_s8], W=[b_s8])
                        S.op("dve", lambda e, xst=xst, xtt=xtt, s8=s8: e.tensor_scalar(out=xst[:], in0=xtt[:], scalar1=s8[:, 3:4], scalar2=None, op0=ALU.mult), R=[b_xt, b_s8], W=[b_xs])
                        tp = 2 * (ti % 2)
                        for k in range(8):
                            S.op("pe", lambda e, k=k, xst=xst, tp=tp: e.transpose(out=pbt[tp + k // 4][:, (k % 4) * 128:(k % 4 + 1) * 128], in_=xst[:, k * 128:(k + 1) * 128], identity=ident_f[:]),
                                 R=[b_xs, b_idf], W=[pbb[tp + k // 4]])
                        for k in range(8):
                            S.op("act", lambda e, k=k, tp=tp, hTt=hTt, j=j: e.activation(out=hTt[:, k, j * 128:(j + 1) * 128], in_=pbt[tp + k // 4][:, (k % 4) * 128:(k % 4 + 1) * 128],
                                                                                   func=AF.Identity, scale=gmulT[:, k:k + 1], bias=modT[:, k:k + 1]),
                                 R=[pbb[tp + k // 4], b_gmulT, b_modT], W=[b_hT])
                        bk = nbank()
                        for k in range(8):
                            S.op("pe", lambda e, k=k, bk=bk, hTt=hTt, j=j: e.matmul(pbt[bk][:, :], lhsT=hTt[:, k, j * 128:(j + 1) * 128], rhs=w_bf[:, k, C_XA:C_XA + 512], start=(k == 0), stop=(k == 7)),
                                 R=[b_hT, b_w], W=[pbb[bk]])
                        S.op("dve", lambda e, bk=bk, xat=xat: e.tensor_copy(out=xat[:], in_=pbt[bk][:, :]), R=[pbb[bk]], W=[b_xa])
                        S.dma("pool", xa_s[tok0:tok0 + 128, :], xat[:], R=[b_xa], W=[db("xa", ti)])
                        for (c0, wd, o0, fn) in tm_groups:
                            bk = nbank()
                            for k in range(8):
                                S.op("pe", lambda e, k=k, bk=bk, hTt=hTt, j=j, c0=c0: e.matmul(pbt[bk][:, :], lhsT=hTt[:, k, j * 128:(j + 1) * 128], rhs=w_bf[:, k, c0:c0 + 512], start=(k == 0), stop=(k == 7)),
                                     R=[b_hT, b_w], W=[pbb[bk]])
                            S.op("act", lambda e, bk=bk, tmt=tmt, o0=o0, fn=fn: e.activation(out=tmt[:, o0:o0 + 512], in_=pbt[bk][:, :], func=fn), R=[pbb[bk]], W=[b_tm])
                        bk = nbank()
                        for k in range(8):
                            S.op("pe", lambda e, k=k, bk=bk, hTt=hTt, j=j: e.matmul(pbt[bk][:, 0:64], lhsT=hTt[:, k, j * 128:(j + 1) * 128], rhs=w_bf[:, k, C_V:C_V + 64], start=(k == 0), stop=(k == 7)),
                                 R=[b_hT, b_w], W=[pbb[bk]])
                        for k in range(8):
                            S.op("pe", lambda e, k=k, bk=bk, hTt=hTt, j=j: e.matmul(pbt[bk][:, 64:72], lhsT=hTt[:, k, j * 128:(j + 1) * 128], rhs=w_bf[:, k, C_WI:C_WI + 8], start=(k == 0), stop=(k == 7)),
                                 R=[b_hT, b_w], W=[pbb[bk]])
                        S.op("dve", lambda e, bk=bk, tmt=tmt: e.tensor_copy(out=tmt[:, 3072:3136], in_=pbt[bk][:, 0:64]), R=[pbb[bk]], W=[b_tm])
                        S.op("dve", lambda e, bk=bk, wit=wit: e.tensor_scalar(out=wit[:], in0=pbt[bk][:, 64:72], scalar1=IDX_SCALE, scalar2=None, op0=ALU.mult), R=[pbb[bk]], W=[b_wi])
                        S.dma("pool", tm_s[tok0:tok0 + 128, :], tmt[:], R=[b_tm], W=[db("tm", ti)])
                        S.dma("pool", wi_s[tok0:tok0 + 128, :], wit[:], R=[b_wi], W=[db("wi", ti)])
                    for i in range(10):
                        bk = nbank()
                        for k in range(8):
                            if i < 4:
                                lw = w_bf[:, k, C_Q + i * 128:C_Q + (i + 1) * 128]
                                rb = b_w
                            elif i < 8:
                                lw = w_bf[:, k, C_QI + (i - 4) * 128:C_QI + (i - 3) * 128]
                                rb = b_w
                            else:
                                lw = wdup[:, k, (i - 8) * 128:(i - 7) * 128]
                                rb = b_wdup
                            S.op("pe", lambda e, k=k, bk=bk, lw=lw, hTt=hTt: e.matmul(pbt[bk][:, :], lhsT=lw, rhs=hTt[:, k, :], start=(k == 0), stop=(k == 7)),
                                 R=[b_hT, rb], W=[pbb[bk]])
                        if i < 4:
                            S.op("act", lambda e, bk=bk, fmt=fmt, i=i: e.mul(out=fmt[:, i, :], in_=pbt[bk][:, :], mul=0.125), R=[pbb[bk]], W=[b_fm])
                        else:
                            S.op("dve", lambda e, bk=bk, fmt=fmt, i=i: e.tensor_copy(out=fmt[:, i, :], in_=pbt[bk][:, :]), R=[pbb[bk]], W=[b_fm])
                    S.dma("pool", fm_s[:, :, st * 512:(st + 1) * 512], fmt[:, :, :], R=[b_fm], W=[db("fm", st)])
                S.run()

        def phase_S(l):
            MA = J + 1
            MD = 2 * J
            NP8 = J // 8
            with ExitStack() as es:
                B = {}
                al = lambda n, s, d: _alloc(nc, es, B, n, s, d)
                araw, b_araw = al("araw", [NG, 4, 128], F32)
                ldt, b_ldt = al("ldt", [NG, 2], F32)
                AT, b_AT = al("AT", [128, 4, NG], F32)
                mta, b_mta = al("mta", [128, MA], F32)
                mtd, b_mtd = al("mtd", [128, MD], F32)
                ang, b_ang = al("ang", [128, NG, MD], F32)
                Emag, b_E = al("Emag", [128, NG, MD], F32)
                scr, b_scr = al("scr", [128, NG, MD], F32)
                iscr, b_iscr = al("iscr", [128, NG, MD], mybir.dt.int32)
                PA1, b_PA1 = al("PA1", [128, NG, MA], F32)
                PA2, b_PA2 = al("PA2", [128, NG, MA], F32)
                TA, b_TA = al("TA", [128, NG, MD], F32)
                TB, b_TB = al("TB", [128, NG, MD], F32)
                sgn, b_sgn = al("sgn", [128, 4], F32)
                ab, b_ab = al("ab", [128, 4, NG], F32)
                cf, b_cf = al("cf", [128, 8, NG], F32)
                Braw, b_Braw = al("Braw", [128, 2, NG, GP], F32)
                BA, b_BA = al("BA", [128, NG, GP], F32)
                BB2, b_BB2 = al("BB2", [128, NG, GP], F32)
                Craw, b_Craw = al("Craw", [128, 4, 2, 128], F32)
                CT1, b_CT1 = al("CT1", [128, NG * GP], F32)
                CT2, b_CT2 = al("CT2", [128, NG * GP], F32)
                kmask, b_kmask = al("kmask", [128, KC, JP], F32)
                Xf = [al("Xf%d" % i, [128, 8, 128], F32) for i in range(2)]
                Xb = [al("Xb%d" % i, [128, 8, J, GP], BF16) for i in range(2)]
                UT = [al("UT%d" % i, [128, KC, 128], BF16) for i in range(2)]
                ZZ, b_ZA = al("ZZ", [128, NG, 2, 128], F32)
                b_ZB = b_ZA
                HA, b_HA = al("HA", [128, NG], F32)
                HB, b_HB = al("HB", [128, NG], F32)
                t1, b_t1 = al("t1", [128, NG], F32)
                t2, b_t2 = al("t2", [128, NG], F32)
                DA1, b_DA1 = al("DA1", [128, NG], F32)
                DA2, b_DA2 = al("DA2", [128, NG], F32)
                DB2, b_DB2 = al("DB2", [128, NG], F32)
                Hp, b_Hp = al("Hp", [128, NG, 128], BF16)
                bt = [al("bt%d" % i, [128, J, GP], F32) for i in range(2)]
                btc = [al("btc%d" % i, [128, J + 1, GP], F32) for i in range(2)]
                wsrc = [al("wsrc%d" % i, [128, J, GP], BF16) for i in range(2)]
                bneg = [al("bneg%d" % i, [128, J, GP], BF16) for i in range(2)]
                ccx = [al("ccx%d" % i, [128, J + 1, GP], BF16) for i in range(2)]
                WBA = [al("WBA%d" % i, [128, KC, 128], BF16) for i in range(2)]
                WBB = [al("WBB%d" % i, [128, KC, 128], BF16) for i in range(2)]
                Kf = [al("Kf%d" % i, [128, KC, JP], BF16) for i in range(2)]
                Yb, b_Yb = al("Yb", [128, J, 128], F32)
                dv = lambda fn, R, W: S.op("dve", fn, R=R, W=W)

                for w, src in enumerate((a_re, a_im)):
                    S.dma("sp", araw[:, w, 0:64], src[l, :, :], W=[b_araw])
                    S.dma("sp", araw[:, w, 64:128], src[l, :, :], W=[b_araw])
                S.dma("sp", ldt[:, 0:1], log_dt[l, :, :], W=[b_ldt])
                S.dma("sp", mta[:], mtab_in[:, 0:MA], W=[b_mta])
                S.dma("sp", mtd[:], mtab_in[:, MA:MA + MD], W=[b_mtd])
                S.dma("sp", kmask[:], kmask_in[:, :, :], W=[b_kmask])
                S.op("act", lambda e: e.activation(out=ldt[:, 1:2], in_=ldt[:, 0:1], func=AF.Exp), R=[b_ldt], W=[b_ldt])
                for w in range(2):
                    dv(lambda e, w=w: e.tensor_scalar(out=araw[:, 2 + w, :], in0=araw[:, w, :], scalar1=ldt[:, 1:2], scalar2=None, op0=ALU.mult), [b_araw, b_ldt], [b_araw])
                for w in range(4):
                    S.op("pe", lambda e, w=w: e.transpose(out=pbt[0][:, w * NG:(w + 1) * NG], in_=araw[:, w, :], identity=ident_f[0:NG, 0:NG]), R=[b_araw, b_idf], W=[pbb[0]])
                dv(lambda e: e.tensor_copy(out=AT[:].rearrange("p w g -> p (w g)"), in_=pbt[0][:, 0:4 * NG]), [pbb[0]], [b_AT])
                dv(lambda e: e.memset(sgn[:], 1.0), [], [b_sgn])
                dv(lambda e: e.memset(sgn[0:64, 1:2], -1.0), [b_sgn], [b_sgn])
                dv(lambda e: e.memset(sgn[64:128, 2:3], -1.0), [b_sgn], [b_sgn])
                TWO_PI = 2.0 * math.pi
                OFFS = math.pi + 256.0 * TWO_PI

                def ptables(mt, b_mt, Mn, PRt, b_PR, PIt, b_PI):
                    a3 = ang[:, :, 0:Mn]
                    e3 = Emag[:, :, 0:Mn]
                    mb3 = mt[:].unsqueeze(1).to_broadcast([128, NG, Mn])
                    dv(lambda e: e.tensor_tensor(out=a3, in0=AT[:, 3, :].unsqueeze(2).to_broadcast([128, NG, Mn]), in1=mb3, op=ALU.mult), [b_AT, b_mt], [b_ang])
                    dv(lambda e: e.tensor_tensor(out=e3, in0=AT[:, 2, :].unsqueeze(2).to_broadcast([128, NG, Mn]), in1=mb3, op=ALU.mult), [b_AT, b_mt], [b_E])
                    S.op("act", lambda e: e.activation(out=e3, in_=e3, func=AF.Exp), R=[b_E], W=[b_E])
                    s3 = scr[:, :, 0:Mn]
                    i3 = iscr[:, :, 0:Mn]
                    for (dst, b_dst, off) in ((PIt, b_PI, 0.0), (PRt, b_PR, 0.25)):
                        dv(lambda e, dst=dst, off=off: e.tensor_scalar(out=dst[:], in0=a3, scalar1=1.0 / TWO_PI, scalar2=off, op0=ALU.mult, op1=ALU.add), [b_ang], [b_dst])
                        dv(lambda e, dst=dst: e.tensor_copy(out=i3, in_=dst[:]), [b_dst], [b_iscr])
                        dv(lambda e: e.tensor_copy(out=s3, in_=i3), [b_iscr], [b_scr])
                        dv(lambda e, dst=dst: e.tensor_tensor(out=dst[:], in0=dst[:], in1=s3, op=ALU.subtract), [b_dst, b_scr], [b_dst])
                        dv(lambda e, dst=dst: e.tensor_scalar(out=s3, in0=dst[:], scalar1=0.5, scalar2=None, op0=ALU.is_gt), [b_dst], [b_scr])
                        dv(lambda e, dst=dst: e.tensor_tensor(out=dst[:], in0=dst[:], in1=s3, op=ALU.subtract), [b_dst, b_scr], [b_dst])
                        dv(lambda e, dst=dst: e.tensor_scalar(out=s3, in0=dst[:], scalar1=-0.5, scalar2=None, op0=ALU.is_lt), [b_dst], [b_scr])
                        dv(lambda e, dst=dst: e.tensor_tensor(out=dst[:], in0=dst[:], in1=s3, op=ALU.add), [b_dst, b_scr], [b_dst])
                        S.op("act", lambda e, dst=dst: e.activation(out=dst[:], in_=dst[:], func=AF.Sin, scale=TWO_PI), R=[b_dst], W=[b_dst])
                    dv(lambda e: e.tensor_tensor(out=PRt[:], in0=PRt[:], in1=e3, op=ALU.mult), [b_PR, b_E], [b_PR])
                    dv(lambda e: e.tensor_tensor(out=PIt[:], in0=PIt[:], in1=e3, op=ALU.mult), [b_PI, b_E], [b_PI])

                ptables(mta, b_mta, MA, PA1, b_PA1, PA2, b_PA2)
                for w, (T, bT, idx) in enumerate(((PA1, b_PA1, 1), (PA2, b_PA2, 1), (PA1, b_PA1, J), (PA2, b_PA2, J))):
                    dv(lambda e, w=w, T=T, idx=idx: e.tensor_copy(out=ab[:, w, :], in_=T[:, :, idx]), [bT], [b_ab])
                dv(lambda e: e.tensor_scalar(out=PA1[:], in0=PA1[:], scalar1=sgn[:, 2:3], scalar2=None, op0=ALU.mult), [b_PA1, b_sgn, b_ab], [b_PA1])
                dv(lambda e: e.tensor_scalar(out=PA2[:], in0=PA2[:], scalar1=-1.0, scalar2=None, op0=ALU.mult), [b_PA2, b_ab], [b_PA2])
                ptables(mtd, b_mtd, MD, TA, b_TA, TB, b_TB)
                dv(lambda e: e.tensor_scalar(out=TB[:], in0=TB[:], scalar1=sgn[:, 1:2], scalar2=None, op0=ALU.mult), [b_TB, b_sgn], [b_TB])
                ar, ai = AT[:, 0, :], AT[:, 1, :]
                abr, abi, Dr, Di = (ab[:, i, :] for i in range(4))
                c_nr, c_den, c_t1, c_t2, c_cr, c_ci, c_rd = (cf[:, i, :] for i in range(7))
                dv(lambda e: e.tensor_scalar(out=c_nr, in0=abr, scalar1=-1.0, scalar2=None, op0=ALU.add), [b_ab], [b_cf])
                dv(lambda e: e.tensor_tensor(out=c_den, in0=ar, in1=ar, op=ALU.mult), [b_AT, b_cf], [b_cf])
                dv(lambda e: e.tensor_tensor(out=c_t1, in0=ai, in1=ai, op=ALU.mult), [b_AT, b_cf], [b_cf])
                dv(lambda e: e.tensor_tensor(out=c_den, in0=c_den, in1=c_t1, op=ALU.add), [b_cf], [b_cf])
                dv(lambda e: e.reciprocal(out=c_rd, in_=c_den), [b_cf], [b_cf])
                dv(lambda e: e.tensor_tensor(out=c_t1, in0=c_nr, in1=ar, op=ALU.mult), [b_cf, b_AT], [b_cf])
                dv(lambda e: e.tensor_tensor(out=c_t2, in0=abi, in1=ai, op=ALU.mult), [b_ab, b_AT, b_cf], [b_cf])
                dv(lambda e: e.tensor_tensor(out=c_t1, in0=c_t1, in1=c_t2, op=ALU.add), [b_cf], [b_cf])
                dv(lambda e: e.tensor_tensor(out=c_cr, in0=c_t1, in1=c_rd, op=ALU.mult), [b_cf], [b_cf])
                dv(lambda e: e.tensor_tensor(out=c_t1, in0=abi, in1=ar, op=ALU.mult), [b_ab, b_AT, b_cf], [b_cf])
                dv(lambda e: e.tensor_tensor(out=c_t2, in0=c_nr, in1=ai, op=ALU.mult), [b_cf, b_AT], [b_cf])
                dv(lambda e: e.tensor_tensor(out=c_t1, in0=c_t1, in1=c_t2, op=ALU.subtract), [b_cf], [b_cf])
                dv(lambda e: e.tensor_tensor(out=c_ci, in0=c_t1, in1=c_rd, op=ALU.mult), [b_cf], [b_cf])
                for (w, lo_, src) in ((0, 0, b_re), (0, 64, b_im), (1, 0, b_im), (1, 64, b_re)):
                    S.dma("sp", Braw[lo_:lo_ + 64, w, :, :], src[l].rearrange("g n q -> n g q"), W=[b_Braw])
                dv(lambda e: e.tensor_scalar(out=c_t2, in0=c_ci, scalar1=sgn[:, 1:2], scalar2=None, op0=ALU.mult), [b_cf, b_sgn], [b_cf])
                bc = lambda a: a.unsqueeze(2).to_broadcast([128, NG, GP])
                dv(lambda e: e.tensor_tensor(out=BA[:], in0=Braw[:, 0, :, :], in1=bc(c_cr), op=ALU.mult), [b_Braw, b_cf], [b_BA])
                dv(lambda e: e.tensor_tensor(out=BB2[:], in0=Braw[:, 1, :, :], in1=bc(c_t2), op=ALU.mult), [b_Braw, b_cf], [b_BB2])
                dv(lambda e: e.tensor_tensor(out=BA[:], in0=BA[:], in1=BB2[:], op=ALU.add), [b_BA, b_BB2], [b_BA])
                dv(lambda e: e.tensor_tensor(out=BB2[:], in0=Braw[:, 1, :, :], in1=bc(c_cr), op=ALU.mult), [b_Braw, b_cf, b_BA], [b_BB2])
                dv(lambda e: e.tensor_scalar(out=c_t2, in0=c_t2, scalar1=-1.0, scalar2=None, op0=ALU.mult), [b_cf, b_BB2], [b_cf])
                dv(lambda e: e.tensor_tensor(out=Braw[:, 1, :, :], in0=Braw[:, 0, :, :], in1=bc(c_t2), op=ALU.mult), [b_Braw, b_cf, b_BB2], [b_Braw])
                dv(lambda e: e.tensor_tensor(out=BB2[:], in0=BB2[:], in1=Braw[:, 1, :, :], op=ALU.add), [b_BB2, b_Braw], [b_BB2])
                for a in range(4):
                    S.dma("sp", Craw[:, a, 0, 0:64], c_re[l, a * 128:(a + 1) * 128, :], W=[b_Craw])
                    S.dma("sp", Craw[:, a, 0, 64:128], c_im[l, a * 128:(a + 1) * 128, :], W=[b_Craw])
                    S.dma("sp", Craw[:, a, 1, 0:64], c_im[l, a * 128:(a + 1) * 128, :], W=[b_Craw])
                    S.dma("sp", Craw[:, a, 1, 64:128], c_re[l, a * 128:(a + 1) * 128, :], W=[b_Craw])
                for w, (CT, bCT) in enumerate(((CT1, b_CT1), (CT2, b_CT2))):
                    for a in range(4):
                        S.op("pe", lambda e, a=a, w=w: e.transpose(out=pbt[1 + w][:, a * 128:(a + 1) * 128], in_=Craw[:, a, w, :], identity=ident_f[:]), R=[b_Craw, b_idf], W=[pbb[1 + w]])
                    dv(lambda e, CT=CT, w=w: e.tensor_copy(out=CT[:], in_=pbt[1 + w][:, :]), [pbb[1 + w]], [bCT])
                dv(lambda e: e.tensor_copy(out=DA1[:], in_=Dr), [b_ab], [b_DA1])
                dv(lambda e: e.tensor_scalar(out=DA2[:], in0=Di, scalar1=sgn[:, 1:2], scalar2=None, op0=ALU.mult), [b_ab, b_sgn], [b_DA2])
                dv(lambda e: e.tensor_scalar(out=DB2[:], in0=Di, scalar1=sgn[:, 2:3], scalar2=None, op0=ALU.mult), [b_ab, b_sgn], [b_DB2])
                dv(lambda e: e.memset(HA[:], 0.0), [], [b_HA])
                dv(lambda e: e.memset(HB[:], 0.0), [], [b_HB])

                SSTOP = int(os.environ.get('S_STOP', '9'))

                def load_x(sc_i, gt, slot):
                    t0 = sc_i * 128 * J
                    xb, b_xb = Xb[slot]
                    xsrc = xa_s[t0:t0 + 128 * J, gt * 128:(gt + 1) * 128].rearrange("(j s) c -> j s c", s=J)
                    for pc in range(NP8):
                        xf, b_xf = Xf[pc % 2]
                        S.dma("sp", xf[:], xsrc[:, pc * 8:(pc + 1) * 8, :], R=[db("xa", (t0 // 128) + i) for i in range(J)], W=[b_xf])
                        S.op("act", lambda e, xf=xf, pc=pc, xb=xb: e.activation(out=xb[:, :, pc * 8:(pc + 1) * 8, :].rearrange("p g s q -> p s g q"), in_=xf[:].rearrange("p s (g q) -> p s g q", q=GP), func=AF.Copy), R=[b_xf], W=[b_xb])
                    return xb, b_xb

                def make_ut(xb, b_xb, gl, slot):
                    ut, b_ut = UT[slot]
                    bk2 = 4 + slot
                    pbf2 = pbt[bk2].ap().bitcast(BF16)
                    for k in range(KC):
                        S.op("pe", lambda e, k=k: e.transpose(out=pbf2[:, k * 128:(k + 1) * 128], in_=xb[:, gl, k * 8:(k + 1) * 8, :].rearrange("p s q -> p (s q)"), identity=ident_b[:]),
                             R=[b_xb, b_idb], W=[pbb[bk2]])
                    S.op("act", lambda e: e.activation(out=ut[:].rearrange("p k n -> p (k n)"), in_=pbf2[:, 0:KC * 128], func=AF.Copy), R=[pbb[bk2]], W=[b_ut])
                    return ut, b_ut

                for sc_i in range(NSC if SSTOP > 1 else 0):
                    t0 = sc_i * 128 * J
                    for gt in range(4):
                        xb, b_xb = load_x(sc_i, gt, gt % 2)
                        SSUB = int(os.environ.get('S_SUB', '9'))
                        for gl in range(8 if SSUB > 0 else 0):
                            g = gt * 8 + gl
                            sl = g % 2
                            btt, b_bt = bt[sl]
                            ws, b_ws = wsrc[sl]
                            wba, b_wba = WBA[sl]
                            wbb, b_wbb = WBB[sl]
                            S.op("dve", lambda e, btt=btt, g=g: e.tensor_tensor(out=btt[:], in0=TA[:, g, 0:J].unsqueeze(2).to_broadcast([128, J, GP]), in1=BA[:, g, :].unsqueeze(1).to_broadcast([128, J, GP]), op=ALU.mult),
                                 R=[b_TA, b_BA], W=[b_bt])
                            S.op("dve", lambda e, ws=ws, g=g: e.tensor_tensor(out=ws[:], in0=TB[:, g, 0:J].unsqueeze(2).to_broadcast([128, J, GP]), in1=BB2[:, g, :].unsqueeze(1).to_broadcast([128, J, GP]), op=ALU.mult),
                                 R=[b_TB, b_BB2], W=[b_ws])
                            S.op("dve", lambda e, ws=ws, btt=btt: e.tensor_tensor(out=ws[:], in0=ws[:], in1=btt[:], op=ALU.add), R=[b_ws, b_bt], W=[b_ws])
                            if SSUB < 2:
                                continue
                            bk = 2 + sl
                            pbf = pbt[bk].ap().bitcast(BF16)
                            for k in range(KC):
                                S.op("pe", lambda e, k=k, ws=ws, pbf=pbf: e.transpose(out=pbf[:, k * 128:(k + 1) * 128], in_=ws[:, k * 8:(k + 1) * 8, :].rearrange("p s q -> p (s q)"), identity=ident_b[:]),
                                     R=[b_ws, b_idb], W=[pbb[bk]])
                            S.op("act", lambda e, wba=wba, pbf=pbf: e.activation(out=wba[:].rearrange("p k n -> p (k n)"), in_=pbf[:, 0:KC * 128], func=AF.Copy), R=[pbb[bk]], W=[b_wba])
                            pb3 = pbf[:, 0:KC * 128].rearrange("p (k n) -> p k n", n=128)
                            dv(lambda e, wbb=wbb, wba=wba: e.tensor_copy(out=wbb[:, :, 0:64], in_=wba[:, :, 64:128]), [b_wba], [b_wbb])
                            dv(lambda e, wbb=wbb, wba=wba: e.tensor_copy(out=wbb[:, :, 64:128], in_=wba[:, :, 0:64]), [b_wba], [b_wbb])
                            if SSUB < 3:
                                continue
                            ut, b_ut = make_ut(xb, b_xb, gl, sl)
                            if SSUB < 4:
                                continue
                            bk3 = 6 + sl
                            for k in range(KC):
                                S.op("pe", lambda e, k=k, wba=wba, bk3=bk3, ut=ut: e.matmul(pbt[bk3][:, 0:128], lhsT=wba[:, k, :], rhs=ut[:, k, :], start=(k == 0), stop=(k == KC - 1)),
                                     R=[b_wba, b_ut], W=[pbb[bk3]])
                            SV = int(os.environ.get('S_V', '0'))
                            for k in range(KC if SV != 1 else 0):
                                S.op("pe", lambda e, k=k, wbb=wbb, bk3=bk3, ut=ut: e.matmul(pbt[bk3][:, 128:256], lhsT=wbb[:, k, :], rhs=ut[:, k, :], start=(k == 0), stop=(k == KC - 1)),
                                     R=[b_wbb, b_ut], W=[pbb[bk3]])
                            dv(lambda e, g=g, bk3=bk3: e.tensor_copy(out=ZZ[:, g, :, :].rearrange("p a j -> p (a j)"), in_=pbt[bk3][:, 0:256]), [pbb[bk3]], [b_ZA])
                    for j in range(128 if SSTOP > 2 else 0):
                        S.op("act", lambda e, j=j: e.activation(out=Hp[:, :, j], in_=HA[:], func=AF.Copy), R=[b_HA], W=[b_Hp])
                        dv(lambda e: e.tensor_tensor(out=t1[:], in0=DA1[:], in1=HA[:], op=ALU.mult), [b_DA1, b_HA], [b_t1])
                        dv(lambda e: e.tensor_tensor(out=t2[:], in0=DA2[:], in1=HB[:], op=ALU.mult), [b_DA2, b_HB], [b_t2])
                        dv(lambda e: e.tensor_tensor(out=t1[:], in0=t1[:], in1=t2[:], op=ALU.add), [b_t1, b_t2], [b_t1])
                        dv(lambda e: e.tensor_tensor(out=t2[:], in0=DA1[:], in1=HB[:], op=ALU.mult), [b_DA1, b_HB, b_t1], [b_t2])
                        dv(lambda e: e.tensor_tensor(out=HB[:], in0=DB2[:], in1=HA[:], op=ALU.mult), [b_DB2, b_HA, b_t2], [b_HB])
                        dv(lambda e: e.tensor_tensor(out=HB[:], in0=HB[:], in1=t2[:], op=ALU.add), [b_HB, b_t2], [b_HB])
                        dv(lambda e, j=j: e.tensor_tensor(out=HB[:], in0=HB[:], in1=ZZ[:, :, 1, j], op=ALU.add), [b_HB, b_ZB], [b_HB])
                        dv(lambda e, j=j: e.tensor_tensor(out=HA[:], in0=t1[:], in1=ZZ[:, :, 0, j], op=ALU.add), [b_t1, b_ZA, b_Hp], [b_HA])
                    for gt in range(4 if SSTOP > 3 else 0):
                        xb, b_xb = load_x(sc_i, gt, gt % 2)
                        for gl in range(8):
                            g = gt * 8 + gl
                            sl = g % 2
                            btt, b_bt = bt[sl]
                            bn, b_bn = bneg[sl]
                            bct, b_bct = btc[sl]
                            cx, b_cx = ccx[sl]
                            kf, b_kf = Kf[sl]
                            S.op("dve", lambda e, btt=btt, g=g: e.tensor_tensor(out=btt[:], in0=TA[:, g, J - 1:2 * J - 1].unsqueeze(2).to_broadcast([128, J, GP]), in1=BA[:, g, :].unsqueeze(1).to_broadcast([128, J, GP]), op=ALU.mult),
                                 R=[b_TA, b_BA], W=[b_bt])
                            S.op("dve", lambda e, bn=bn, g=g: e.tensor_tensor(out=bn[:], in0=TB[:, g, J - 1:2 * J - 1].unsqueeze(2).to_broadcast([128, J, GP]), in1=BB2[:, g, :].unsqueeze(1).to_broadcast([128, J, GP]), op=ALU.mult),
                                 R=[b_TB, b_BB2], W=[b_bn])
                            S.op("dve", lambda e, bn=bn, btt=btt: e.tensor_tensor(out=bn[:], in0=bn[:], in1=btt[:], op=ALU.add), R=[b_bn, b_bt], W=[b_bn])
                            S.op("dve", lambda e, bct=bct, g=g: e.tensor_tensor(out=bct[:], in0=PA1[:, g, 0:J + 1].unsqueeze(2).to_broadcast([128, J + 1, GP]),
                                                                               in1=CT1[:, g * GP:(g + 1) * GP].unsqueeze(1).to_broadcast([128, J + 1, GP]), op=ALU.mult),
                                 R=[b_PA1, b_CT1], W=[b_bct])
                            S.op("dve", lambda e, cx=cx, g=g: e.tensor_tensor(out=cx[:], in0=PA2[:, g, 0:J + 1].unsqueeze(2).to_broadcast([128, J + 1, GP]),
                                                                             in1=CT2[:, g * GP:(g + 1) * GP].unsqueeze(1).to_broadcast([128, J + 1, GP]), op=ALU.mult),
                                 R=[b_PA2, b_CT2], W=[b_cx])
                            S.op("dve", lambda e, cx=cx, bct=bct: e.tensor_tensor(out=cx[:], in0=cx[:], in1=bct[:], op=ALU.add), R=[b_cx, b_bct], W=[b_cx])
                            ut, b_ut = make_ut(xb, b_xb, gl, sl)
                            for k in range(KC):
                                bk = (k % 2)
                                for nn in range(0, JP, 512):
                                    nw = min(512, JP - nn)
                                    S.op("pe", lambda e, k=k, bk=bk, bn=bn, cx=cx, nn=nn, nw=nw: e.matmul(pbt[bk][:, 0:nw], lhsT=bn[:, k * 8:(k + 1) * 8, :].rearrange("p s q -> p (s q)"),
                                                                                                 rhs=cx[:, 0:J, :].rearrange("p r q -> p (r q)")[:, nn:nn + nw], start=True, stop=True),
                                         R=[b_bn, b_cx], W=[pbb[bk]])
                                    dv(lambda e, k=k, bk=bk, kf=kf, nn=nn, nw=nw: e.tensor_tensor(out=kf[:, k, nn:nn + nw], in0=pbt[bk][:, 0:nw], in1=kmask[:, k, nn:nn + nw], op=ALU.mult),
                                       [pbb[bk], b_kmask], [b_kf])
                            for nn in range(0, JP, 512):
                                nw = min(512, JP - nn)
                                bk = 2 + sl
                                for k in range(KC):
                                    S.op("pe", lambda e, k=k, bk=bk, ut=ut, kf=kf, nn=nn, nw=nw: e.matmul(pbt[bk][:, 0:nw], lhsT=ut[:, k, :], rhs=kf[:, k, nn:nn + nw], start=(k == 0), stop=False),
                                         R=[b_ut, b_kf], W=[pbb[bk]])
                                S.op("pe", lambda e, bk=bk, g=g, cx=cx, nn=nn, nw=nw: e.matmul(pbt[bk][:, 0:nw], lhsT=Hp[:, g, :], rhs=cx[:, 1:J + 1, :].rearrange("p r q -> p (r q)")[:, nn:nn + nw], start=False, stop=True),
                                     R=[b_Hp, b_cx], W=[pbb[bk]])
                                r0, rn = nn // GP, nw // GP
                                S.op("act", lambda e, bk=bk, gl=gl, r0=r0, rn=rn, nw=nw: e.activation(out=Yb[:, r0:r0 + rn, gl * GP:(gl + 1) * GP], in_=pbt[bk][:, 0:nw].rearrange("p (r q) -> p r q", q=GP), func=AF.Copy),
                                     R=[pbb[bk]], W=[b_Yb])
                        ydst = ys_s[t0:t0 + 128 * J, gt * 128:(gt + 1) * 128].rearrange("(j s) c -> j s c", s=J)
                        S.dma("pool", ydst, Yb[:, :, :], R=[b_Yb], W=[db("ys", sc_i, gt)])
                S.run()

        def phase_D(l):
            NKT = L // 128
            with ExitStack() as es:
                B = {}
                al = lambda n, s, d: _alloc(nc, es, B, n, s, d)
                KI, b_KI = al("KI", [128, L], BF16)
                KK, b_KK = al("KK", [128, L], BF16)
                V1, b_V1 = al("V1", [128, NKT, 128], BF16)
                cb, b_cb = al("cb", [128, 128], F32)
                score, b_score = al("score", [128, L], F32)
                junk, b_junk = al("junkD", [128, L], BF16)
                maskb, b_maskb = al("maskb", [128, L], BF16)
                maskT, b_maskT = al("maskT", [128, NKT, 128], BF16)
                qiT = [al("qiT%d" % i, [128, 4, 128], BF16) for i in range(2)]
                qT = [al("qT%d" % i, [128, 4, 128], BF16) for i in range(2)]
                wiq = [al("wiq%d" % i, [128, 8], F32) for i in range(2)]
                Rt = [al("Rt%d" % i, [128, 512], F32) for i in range(2)]
                Pt = [al("Pt%d" % i, [128, 8, 128], BF16) for i in range(2)]
                Pm = [al("Pm%d" % i, [128, 8, 128], BF16) for i in range(2)]
                sm, b_sm = al("sm", [128, 16], F32)
                zt, b_zt = al("zt", [128, 512], BF16)
                S.op("dve", lambda e: e.memset(zt[:], 0.0), W=[b_zt])
                osb, b_osb = al("osb", [128, 8, 65], F32)
                rsum, b_rsum = al("rsum", [128, 8], F32)
                ost = [al("ost%d" % i, [128, 512], BF16) for i in range(2)]

                S.dma("sp", KI[:], fm_s[:, 9, :], R=[db("fm", i) for i in range(NST)], W=[b_KI])
                S.dma("sp", KK[:], fm_s[:, 8, :], R=[db("fm", i) for i in range(NST)], W=[b_KK])
                S.dma("sp", V1[:, :, 0:64], tm_s[:, 3072:3136].rearrange("(t p) d -> p t d", p=128), R=[db("tm", i) for i in range(NQB)], W=[b_V1])
                S.op("dve", lambda e: e.memset(V1[:, :, 64:65], 1.0), W=[b_V1])
                S.op("dve", lambda e: e.memset(V1[:, :, 65:128], 0.0), W=[b_V1])
                S.dma("sp", cb[:], causal_in[:, :], W=[b_cb])
                dv = lambda fn, R, W: S.op("dve", fn, R=R, W=W)
                lo, hi, mid, cnt, ge, dd, nge, ee = (sm[:, i:i + 1] for i in range(8))
                DSTOP = int(os.environ.get('D_STOP', '9'))
                for qb in range(min(NQB, int(os.environ.get('D_NQB', '9999')))):
                    nk = 128 * (qb + 1)
                    q0 = qb * 128
                    qit, b_qi = qiT[qb % 2]
                    qt, b_q = qT[qb % 2]
                    wit, b_wi = wiq[qb % 2]
                    S.dma("sp", qit[:], fm_s[:, 4:8, q0:q0 + 128], R=[db("fm", q0 // 512)], W=[b_qi])
                    S.dma("sp", qt[:], fm_s[:, 0:4, q0:q0 + 128], R=[db("fm", q0 // 512)], W=[b_q])
                    S.dma("sp", wit[:], wi_s[q0:q0 + 128, :], R=[db("wi", qb)], W=[b_wi])
                    ci = 0
                    for k0 in range(0, nk, 512):
                        kw = min(512, nk - k0)
                        for h in range(8):
                            hp, par = h // 2, h % 2
                            bk = 6 + (ci % 2)
                            rt, b_rt = Rt[ci % 2]
                            ci += 1
                            S.op("pe", lambda e, bk=bk, qit=qit, hp=hp, par=par, k0=k0, kw=kw: e.matmul(pbt[bk][:, 0:kw], lhsT=qit[64 * par:64 * par + 64, hp, :], rhs=KI[64 * par:64 * par + 64, k0:k0 + kw], start=True, stop=True),
                                 R=[b_qi, b_KI], W=[pbb[bk]])
                            S.op("act", lambda e, bk=bk, rt=rt, kw=kw: e.activation(out=rt[:, 0:kw], in_=pbt[bk][:, 0:kw], func=AF.Relu), R=[pbb[bk]], W=[b_rt])
                            if h == 0:
                                dv(lambda e, rt=rt, wit=wit, k0=k0, kw=kw: e.tensor_scalar(out=score[:, k0:k0 + kw], in0=rt[:, 0:kw], scalar1=wit[:, 0:1], scalar2=None, op0=ALU.mult),
                                   [b_rt, b_wi], [b_score])
                            else:
                                dv(lambda e, rt=rt, wit=wit, k0=k0, kw=kw, h=h: e.scalar_tensor_tensor(out=score[:, k0:k0 + kw], in0=rt[:, 0:kw], scalar=wit[:, h:h + 1], in1=score[:, k0:k0 + kw], op0=ALU.mult, op1=ALU.add),
                                   [b_rt, b_wi, b_score], [b_score])
                    if DSTOP < 2:
                        continue
                    dv(lambda e, nk=nk: e.tensor_reduce(out=lo, in_=score[:, 0:nk], axis=AX.X, op=ALU.min), [b_score], [b_sm])
                    dv(lambda e, nk=nk: e.tensor_tensor(out=score[:, nk - 128:nk], in0=score[:, nk - 128:nk], in1=cb[:], op=ALU.add), [b_score, b_cb], [b_score])
                    dv(lambda e, nk=nk: e.tensor_reduce(out=hi, in_=score[:, 0:nk], axis=AX.X, op=ALU.max), [b_score], [b_sm])
                    for it in range(nit):
                        dv(lambda e: e.tensor_tensor(out=mid, in0=lo, in1=hi, op=ALU.add), [b_sm], [b_sm])
                        dv(lambda e: e.tensor_scalar(out=mid, in0=mid, scalar1=0.5, scalar2=None, op0=ALU.mult), [b_sm], [b_sm])
                        dv(lambda e: e.memset(cnt, 0.0), [b_sm], [b_sm])
                        dv(lambda e, nk=nk: e.tensor_scalar(out=junk[:, 0:nk], in0=score[:, 0:nk], scalar1=mid, scalar2=0.0, op0=ALU.is_ge, op1=ALU.add, accum_out=cnt), [b_score, b_sm], [b_junk, b_sm])
                        dv(lambda e: e.tensor_scalar(out=ge, in0=cnt, scalar1=float(TOPK), scalar2=None, op0=ALU.is_ge), [b_sm], [b_sm])
                        dv(lambda e: e.tensor_scalar(out=nge, in0=ge, scalar1=-1.0, scalar2=1.0, op0=ALU.mult, op1=ALU.add), [b_sm], [b_sm])
                        dv(lambda e: e.tensor_tensor(out=dd, in0=mid, in1=lo, op=ALU.subtract), [b_sm], [b_sm])
                        dv(lambda e: e.tensor_tensor(out=ee, in0=mid, in1=hi, op=ALU.subtract), [b_sm], [b_sm])
                        dv(lambda e: e.scalar_tensor_tensor(out=lo, in0=dd, scalar=ge, in1=lo, op0=ALU.mult, op1=ALU.add), [b_sm], [b_sm])
                        dv(lambda e: e.scalar_tensor_tensor(out=hi, in0=ee, scalar=nge, in1=hi, op0=ALU.mult, op1=ALU.add), [b_sm], [b_sm])
                    dv(lambda e, nk=nk: e.tensor_scalar(out=maskb[:, 0:nk], in0=score[:, 0:nk], scalar1=lo, scalar2=None, op0=ALU.is_ge), [b_score, b_sm], [b_maskb])
                    if DSTOP < 3:
                        continue
                    for kt0 in range(0, qb + 1, 8):
                        ktn = min(8, qb + 1 - kt0)
                        bk = 6 + ((kt0 // 8) % 2)
                        pbf = pbt[bk].ap().bitcast(BF16)
                        for i in range(ktn):
                            S.op("pe", lambda e, i=i, kt0=kt0, pbf=pbf: e.transpose(out=pbf[:, i * 128:(i + 1) * 128], in_=maskb[:, (kt0 + i) * 128:(kt0 + i + 1) * 128], identity=ident_b[:]),
                                 R=[b_maskb, b_idb], W=[pbb[bk]])
                        S.op("act", lambda e, kt0=kt0, ktn=ktn, pbf=pbf: e.activation(out=maskT[:, kt0:kt0 + ktn, :].rearrange("p t q -> p (t q)"), in_=pbf[:, 0:ktn * 128], func=AF.Copy), R=[pbb[bk]], W=[b_maskT])
                    if DSTOP < 4:
                        continue
                    for bkz in (4, 5):
                        S.op("pe", lambda e, bkz=bkz: e.matmul(pbt[bkz][:, :], lhsT=zt[:, 0:128], rhs=zt[:, :], start=True, stop=True), R=[b_zt], W=[pbb[bkz]])
                    for kt in range(qb + 1):
                        b0 = 2 * (kt % 2)
                        pt, b_pt = Pt[kt % 2]
                        pm, b_pm = Pm[kt % 2]
                        for sidx in range(8):
                            h = (0, 2, 4, 6, 1, 3, 5, 7)[sidx]
                            hp, par = h // 2, h % 2
                            bk = b0 + sidx // 4
                            S.op("pe", lambda e, bk=bk, h=sidx, hp=hp, par=par, kt=kt, qt=qt: e.matmul(pbt[bk][:, (h % 4) * 128:(h % 4 + 1) * 128], lhsT=KK[64 * par:64 * par + 64, kt * 128:(kt + 1) * 128], rhs=qt[64 * par:64 * par + 64, hp, :], start=True, stop=True),
                                 R=[b_KK, b_q], W=[pbb[bk]])
                        DATT = int(os.environ.get('D_ATT', '9'))
                        if DATT < 1:
                            continue
                        for hf in range(2):
                            S.op("act", lambda e, hf=hf, b0=b0, pt=pt: e.activation(out=pt[:, hf * 4:(hf + 1) * 4, :].rearrange("p h q -> p (h q)"), in_=pbt[b0 + hf][:, :], func=AF.Exp), R=[pbb[b0 + hf]], W=[b_pt])
                        if DATT < 2:
                            continue
                        dv(lambda e, pt=pt, pm=pm, kt=kt: e.tensor_tensor(out=pm[:], in0=pt[:], in1=maskT[:, kt, :].unsqueeze(1).to_broadcast([128, 8, 128]), op=ALU.mult), [b_pt, b_maskT], [b_pm])
                        if DATT < 3:
                            continue
                        for sidx in range(8):
                            h = (0, 2, 4, 6, 1, 3, 5, 7)[sidx]
                            bk = 4 + h // 4
                            S.op("pe", lambda e, bk=bk, h=h, sidx=sidx, pm=pm, kt=kt, qb=qb: e.matmul(pbt[bk][:, (h % 4) * 128:(h % 4) * 128 + 65], lhsT=pm[:, sidx, :], rhs=V1[:, kt, 0:65], start=False, stop=(kt == qb)),
                                 R=[b_pm, b_V1], W=[pbb[bk]])
                    if DSTOP < 5:
                        continue
                    for hf in range(2):
                        dv(lambda e, hf=hf: e.tensor_copy(out=osb[:, hf * 4:(hf + 1) * 4, :], in_=pbt[4 + hf][:, :].rearrange("p (h c) -> p h c", c=128)[:, :, 0:65]), [pbb[4 + hf]], [b_osb])
                    dv(lambda e: e.reciprocal(out=rsum[:], in_=osb[:, :, 64]), [b_osb], [b_rsum])
                    ot, b_ot = ost[qb % 2]
                    dv(lambda e, ot=ot: e.tensor_tensor(out=ot[:].rearrange("p (h d) -> p h d", d=64), in0=osb[:, :, 0:64], in1=rsum[:].unsqueeze(2).to_broadcast([128, 8, 64]), op=ALU.mult), [b_osb, b_rsum], [b_ot])
                    S.dma("pool", o_s[q0:q0 + 128, :], ot[:], R=[b_ot], W=[db("o", qb)])
                S.run()

        def phase_O(l, last):
            with ExitStack() as es:
                B = {}
                al = lambda n, s, d: _alloc(nc, es, B, n, s, d)
                wst = [al("wstO%d" % i, [128, D], F32) for i in range(2)]
                wglu, b_wglu = al("wglu", [128, 4, D], BF16)
                wao, b_wao = al("wao", [128, 4, D], BF16)
                wbo, b_wbo = al("wbo", [128, 4, D], BF16)
                wout, b_wout = al("wout", [128, 8, D], BF16)
                dsB, b_dsB = al("dsB", [128, SW], F32)
                bgB, b_bgB = al("bgB", [128, D], F32)
                fgB, b_fgB = al("fgB", [128, D], F32)
                ysT = [al("ysT%d" % i, [128, SW], F32) for i in range(2)]
                xaT = [al("xaT%d" % i, [128, SW], F32) for i in range(2)]
                tmT = [al("tmT%d" % i, [128, 3072], BF16) for i in range(2)]
                oT = [al("oT%d" % i, [128, SW], BF16) for i in range(2)]
                xT = [al("xT%d" % i, [128, D], F32) for i in range(2)]
                gl, b_gl = al("gl", [128, SW], BF16)
                tr4, b_tr4 = al("tr4", [128, 4, 128], BF16)
                tr8, b_tr8 = al("tr8", [128, 8, 128], BF16)
                u1, b_u1 = al("u1", [128, SW], F32)
                u2, b_u2 = al("u2", [128, SW], F32)
                yab, b_yab = al("yab", [128, SW], BF16)
                m1, b_m1 = al("m1", [128, D], F32)
                m2, b_m2 = al("m2", [128, D], F32)
                mb, b_mb = al("mb", [128, D], BF16)
                xn = [al("xn%d" % i, [128, D], F32) for i in range(2)]
                junk, b_junk = al("junkO", [128, D], BF16)
                s8, b_s8 = al("s8O", [128, 8], F32)
                dv = lambda fn, R, W: S.op("dve", fn, R=R, W=W)
                wi = 0
                for (src, dst, bd, nkc) in ((w_glu, wglu, b_wglu, 4), (w_a_o, wao, b_wao, 4), (w_b_o, wbo, b_wbo, 4), (w_out, wout, b_wout, 8)):
                    for k in range(nkc):
                        t, b = wst[wi % 2]
                        wi += 1
                        S.dma("sp", t[:], src[l, k * 128:(k + 1) * 128, :], W=[b])
                        S.op("act", lambda e, t=t, dst=dst, k=k: e.activation(out=dst[:, k, :], in_=t[:], func=AF.Copy), R=[b], W=[bd])
                S.dma("sp", dsB[:], d_skip[l].partition_broadcast(128), W=[b_dsB])
                S.dma("sp", bgB[:], b_glu[l].partition_broadcast(128), W=[b_bgB])
                S.dma("sp", fgB[:], final_g.partition_broadcast(128), W=[b_fgB])
                xsrc = x_in if l == 0 else x1

                def transposes(src_t, b_src, n, dst_t, b_dst, bk):
                    pbf = pbt[bk].ap().bitcast(BF16)
                    for k in range(n):
                        S.op("pe", lambda e, k=k: e.transpose(out=pbf[:, k * 128:(k + 1) * 128], in_=src_t[:, k * 128:(k + 1) * 128], identity=ident_b[:]), R=[b_src, b_idb], W=[pbb[bk]])
                    S.op("act", lambda e: e.activation(out=dst_t[:].rearrange("p k n -> p (k n)"), in_=pbf[:, 0:n * 128], func=AF.Copy), R=[pbb[bk]], W=[b_dst])

                for ti in range(NQB):
                    tok0 = ti * 128
                    sl = ti % 2
                    yst, b_ys = ysT[sl]
                    xat, b_xa = xaT[sl]
                    tmt, b_tm = tmT[sl]
                    ot, b_o = oT[sl]
                    xt, b_x = xT[sl]
                    xnt, b_xn = xn[sl]
                    sc_i, jj = divmod(ti * 128, 128 * J)
                    S.dma("sp", yst[:], ys_s[tok0:tok0 + 128, :], R=[db("ys", sc_i, g4) for g4 in range(4)], W=[b_ys])
                    S.dma("sp", xat[:], xa_s[tok0:tok0 + 128, :], R=[db("xa", ti)], W=[b_xa])
                    S.dma("sp", tmt[:], tm_s[tok0:tok0 + 128, 0:3072], R=[db("tm", ti)], W=[b_tm])
                    S.dma("sp", ot[:], o_s[tok0:tok0 + 128, :], R=[db("o", ti)], W=[b_o])
                    S.dma("sp", xt[:], xsrc[tok0:tok0 + 128, :], R=[db("x", l, ti)], W=[b_x])
                    dv(lambda e, xat=xat: e.tensor_tensor(out=u1[:], in0=xat[:], in1=dsB[:], op=ALU.mult), [b_xa, b_dsB], [b_u1])
                    dv(lambda e, yst=yst: e.tensor_tensor(out=u1[:], in0=u1[:], in1=yst[:], op=ALU.add), [b_u1, b_ys], [b_u1])
                    S.op("act", lambda e: e.activation(out=gl[:], in_=u1[:], func=AF.Gelu), R=[b_u1], W=[b_gl])
                    transposes(gl, b_gl, 4, tr4, b_tr4, 0)
                    for hf in range(2):
                        for k in range(4):
                            S.op("pe", lambda e, hf=hf, k=k: e.matmul(pbt[1 + hf][:, :], lhsT=tr4[:, k, :], rhs=wglu[:, k, hf * 512:(hf + 1) * 512], start=(k == 0), stop=(k == 3)), R=[b_tr4, b_wglu], W=[pbb[1 + hf]])
                    dv(lambda e: e.tensor_tensor(out=u1[:], in0=pbt[1][:, :], in1=bgB[:, 0:512], op=ALU.add), [pbb[1], b_bgB], [b_u1])
                    dv(lambda e: e.tensor_tensor(out=u2[:], in0=pbt[2][:, :], in1=bgB[:, 512:1024], op=ALU.add), [pbb[2], b_bgB], [b_u2])
                    S.op("act", lambda e: e.activation(out=u2[:], in_=u2[:], func=AF.Sigmoid), R=[b_u2], W=[b_u2])
                    dv(lambda e: e.tensor_tensor(out=u1[:], in0=u1[:], in1=u2[:], op=ALU.mult), [b_u1, b_u2], [b_u1])
                    dv(lambda e, tmt=tmt: e.tensor_tensor(out=yab[:], in0=u1[:], in1=tmt[:, 0:512], op=ALU.mult), [b_u1, b_tm], [b_yab])
                    transposes(yab, b_yab, 4, tr4, b_tr4, 0)
                    for hf in range(2):
                        for k in range(4):
                            S.op("pe", lambda e, hf=hf, k=k: e.matmul(pbt[3 + hf][:, :], lhsT=tr4[:, k, :], rhs=wao[:, k, hf * 512:(hf + 1) * 512], start=(k == 0), stop=(k == 3)), R=[b_tr4, b_wao], W=[pbb[3 + hf]])
                    for hf in range(2):
                        dv(lambda e, hf=hf, tmt=tmt: e.tensor_tensor(out=m1[:, hf * 512:(hf + 1) * 512], in0=pbt[3 + hf][:, :], in1=tmt[:, 1024 + hf * 512:1024 + (hf + 1) * 512], op=ALU.mult), [pbb[3 + hf], b_tm], [b_m1])
                    dv(lambda e, ot=ot, tmt=tmt: e.tensor_tensor(out=yab[:], in0=ot[:], in1=tmt[:, 512:1024], op=ALU.mult), [b_o, b_tm, b_tr4], [b_yab])
                    transposes(yab, b_yab, 4, tr4, b_tr4, 0)
                    for hf in range(2):
                        for k in range(4):
                            S.op("pe", lambda e, hf=hf, k=k: e.matmul(pbt[5 + hf][:, :], lhsT=tr4[:, k, :], rhs=wbo[:, k, hf * 512:(hf + 1) * 512], start=(k == 0), stop=(k == 3)), R=[b_tr4, b_wbo], W=[pbb[5 + hf]])
                    for hf in range(2):
                        dv(lambda e, hf=hf, tmt=tmt: e.tensor_tensor(out=m2[:, hf * 512:(hf + 1) * 512], in0=pbt[5 + hf][:, :], in1=tmt[:, 2048 + hf * 512:2048 + (hf + 1) * 512], op=ALU.mult), [pbb[5 + hf], b_tm], [b_m2])
                    dv(lambda e: e.tensor_tensor(out=mb[:], in0=m1[:], in1=m2[:], op=ALU.add), [b_m1, b_m2], [b_mb])
                    pbf7 = pbt[7].ap().bitcast(BF16)
                    for k in range(8):
                        S.op("pe", lambda e, k=k: e.transpose(out=pbf7[:, k * 128:(k + 1) * 128], in_=mb[:, k * 128:(k + 1) * 128], identity=ident_b[:]), R=[b_mb, b_idb], W=[pbb[7]])
                    S.op("act", lambda e: e.activation(out=tr8[:].rearrange("p k n -> p (k n)"), in_=pbf7[:, 0:1024], func=AF.Copy), R=[pbb[7]], W=[b_tr8])
                    for hf in range(2):
                        for k in range(8):
                            S.op("pe", lambda e, hf=hf, k=k: e.matmul(pbt[1 + hf][:, :], lhsT=tr8[:, k, :], rhs=wout[:, k, hf * 512:(hf + 1) * 512], start=(k == 0), stop=(k == 7)), R=[b_tr8, b_wout], W=[pbb[1 + hf]])
                    for hf in range(2):
                        dv(lambda e, hf=hf: e.tensor_tensor(out=m1[:, hf * 512:(hf + 1) * 512], in0=pbt[1 + hf][:, :], in1=gateB[:, hf * 512:(hf + 1) * 512], op=ALU.mult), [pbb[1 + hf], b_gate], [b_m1])
                    dv(lambda e, xt=xt, xnt=xnt: e.tensor_tensor(out=xnt[:], in0=m1[:], in1=xt[:], op=ALU.add), [b_m1, b_x], [b_xn])
                    if not last:
                        S.dma("pool", x1[tok0:tok0 + 128, :], xnt[:], R=[b_xn], W=[db("x", l + 1, ti)])
                    else:
                        dv(lambda e: e.memset(s8[:], 0.0), [], [b_s8])
                        S.op("act", lambda e, xnt=xnt: e.activation(out=junk[:], in_=xnt[:], func=AF.Square, accum_out=s8[:, 0:1]), R=[b_xn, b_s8], W=[b_junk, b_s8])
                        dv(lambda e: e.tensor_scalar(out=s8[:, 1:2], in0=s8[:, 0:1], scalar1=1.0 / D, scalar2=EPS, op0=ALU.mult, op1=ALU.add), [b_s8], [b_s8])
                        S.op("act", lambda e: e.activation(out=s8[:, 2:3], in_=s8[:, 1:2], func=AF.Sqrt), R=[b_s8], W=[b_s8])
                        dv(lambda e: e.reciprocal(out=s8[:, 3:4], in_=s8[:, 2:3]), [b_s8], [b_s8])
                        dv(lambda e, xnt=xnt: e.scalar_tensor_tensor(out=xnt[:], in0=xnt[:], scalar=s8[:, 3:4], in1=fgB[:], op0=ALU.mult, op1=ALU.mult), [b_xn, b_s8, b_fgB], [b_xn])
                        S.dma("pool", out[tok0:tok0 + 128, :], xnt[:], R=[b_xn], W=[db("out", ti)])
                S.run()

        for l in range(depth):
            if "A" in phases:
                phase_A(l)
            if "S" in phases:
                phase_S(l)
            S.new_epoch()
            if "D" in phases:
                phase_D(l)
            S.new_epoch()
            if "O" in phases:
                phase_O(l, l == depth - 1)
            if l < depth - 1:
                S.new_epoch()
    return nc


def host_consts(J):
    JP = J * GP
    KC = JP // 128
    ident = np.eye(128, dtype=np.float32)
    qq = np.arange(128)[:, None]
    kk = np.arange(128)[None, :]
    causal = np.where(kk <= qq, 0.0, NEG).astype(np.float32)
    kmask = np.zeros((128, KC, JP), np.float32)
    for k in range(KC):
        s = 8 * k + (np.arange(128) // GP)
        r = np.arange(JP) // GP
        kmask[:, k, :] = (r[None, :] >= s[:, None]).astype(np.float32)
    mrow = np.concatenate([np.arange(0, J + 1), np.arange(J - 1, -J - 1, -1)]).astype(np.float32)
    mtab = np.tile(mrow[None, :], (128, 1))
    return {"ident": ident, "causal": causal, "kmask": kmask, "mtab": mtab}


def make_in_maps(inputs, L, depth, J, ncores=8):
    f = lambda a: np.ascontiguousarray(np.asarray(a, dtype=np.float32))
    consts = host_consts(J)
    shared = {
        "normg_lay": f(np.asarray(inputs["norm_g"])[:depth].reshape(depth, 8, 128).transpose(0, 2, 1)),
        "w_mod": f(inputs["w_mod"])[:depth], "b_mod": f(inputs["b_mod"])[:depth], "w_in": f(inputs["w_in"])[:depth],
        "a_re": f(inputs["a_re"])[:depth], "a_im": f(inputs["a_im"])[:depth],
        "b_re": f(inputs["b_re"])[:depth], "b_im": f(inputs["b_im"])[:depth],
        "c_re": f(np.asarray(inputs["c_re"])[:depth].reshape(depth, NG * GP, NS)),
        "c_im": f(np.asarray(inputs["c_im"])[:depth].reshape(depth, NG * GP, NS)),
        "d_skip": f(inputs["d_skip"])[:depth], "log_dt": f(np.asarray(inputs["log_dt"])[:depth].reshape(depth, NG, 1)),
        "w_glu": f(inputs["w_glu"])[:depth], "b_glu": f(inputs["b_glu"])[:depth],
        "w_a_o": f(inputs["w_a_o"])[:depth], "w_b_o": f(inputs["w_b_o"])[:depth], "w_out": f(inputs["w_out"])[:depth],
        "final_g": f(inputs["final_g"]),
    }
    shared.update(consts)
    x = np.asarray(inputs["x"], dtype=np.float32)
    c = np.asarray(inputs["c"], dtype=np.float32)
    maps = []
    for core in range(ncores):
        b = core % x.shape[0]
        m = dict(shared)
        m["x"] = np.ascontiguousarray(x[b, :L])
        m["c_lay"] = np.ascontiguousarray(c[b].reshape(8, 128).T)
        maps.append(m)
    return maps


def kernel(**inputs):
    L = 8192
    nc = build(L=L, depth=DEPTH, J=32, nit=16)
    maps = make_in_maps(inputs, L, DEPTH, 32, ncores=8)
    res = run_bass_kernel_spmd(nc, maps, core_ids=list(range(8)))
    outs = [np.asarray(res.results[b]["out"], dtype=np.float32) for b in range(NB)]
    return np.stack(outs, axis=0)
```
